# Optimizing a Trainium2 kernel written in Bass

```python
import math
import jax, jax.numpy as jnp
from jax import lax
import numpy as np

D_MODEL = 1024
BATCH = 8
SEQ = 4096
DEPTH = 4

CHUNK = 64
N_MIXERS = 4
EPS = 1e-6
GLA_HEADS = 4
GLA_DK = (D_MODEL // 2) // GLA_HEADS
GLA_DV = D_MODEL // GLA_HEADS
GLA_GATE_RANK = 16
GLA_TAU = 16.0
GLA_IN = 2 * GLA_HEADS * GLA_DK + 2 * GLA_HEADS * GLA_DV + GLA_GATE_RANK
POOL_WINDOWS = (2, 4, 8, 16)
POOL_GROUPS = len(POOL_WINDOWS)
POOL_GW = D_MODEL // POOL_GROUPS
SSD_DINNER = 2 * D_MODEL
SSD_HEADDIM = 64
SSD_HEADS = SSD_DINNER // SSD_HEADDIM
SSD_GROUPS = 4
SSD_HPG = SSD_HEADS // SSD_GROUPS
SSD_STATE = 128
SSD_CONV = 4
SSD_CONV_DIM = SSD_DINNER + 2 * SSD_GROUPS * SSD_STATE
SSD_IN = SSD_DINNER + SSD_CONV_DIM + SSD_HEADS
SB_HEADS = 16
SB_HEADDIM = D_MODEL // SB_HEADS
SB_QBLOCK = 128
FFN_DIM = 2816
FFN_CONV = 3

kernel_name = 'hybrid_chunk_causal_encoder'


def rms_normalize(x):
    xf = x.astype(jnp.float32)
    y = xf * lax.rsqrt(jnp.mean(xf * xf, axis=-1, keepdims=True) + EPS)
    return y.astype(x.dtype)


def rmsnorm(x, w):
    return rms_normalize(x) * w


def causal_dwconv(u, w, b):
    K = w.shape[0]
    L = u.shape[1]
    up = jnp.pad(u, ((0, 0), (K - 1, 0), (0, 0)))
    out = up[:, K - 1:K - 1 + L] * w[K - 1] + b
    for k in range(K - 1):
        out = out + up[:, k:k + L] * w[k]
    return out


def to_chunks(a):
    b_, l_ = a.shape[:2]
    return jnp.moveaxis(a.reshape(b_, l_ // CHUNK, CHUNK, *a.shape[2:]), 1, 0)


def from_chunks(a):
    a = jnp.moveaxis(a, 0, 1)
    return a.reshape(a.shape[0], a.shape[1] * a.shape[2], *a.shape[3:])


def gla_mixer(h, w_in, w_gate2, b_gate, norm_w, w_out):
    b_, l_, _ = h.shape
    hk, hv = GLA_HEADS * GLA_DK, GLA_HEADS * GLA_DV
    proj = h @ w_in
    q, k, v, r, glr = jnp.split(proj, [hk, 2 * hk, 2 * hk + hv, 2 * hk + 2 * hv], axis=-1)
    q = q.reshape(b_, l_, GLA_HEADS, GLA_DK).astype(jnp.float32) * (GLA_DK ** -0.5)
    k = k.reshape(b_, l_, GLA_HEADS, GLA_DK).astype(jnp.float32)
    v = v.reshape(b_, l_, GLA_HEADS, GLA_DV).astype(jnp.float32)
    log_a = jax.nn.log_sigmoid((glr @ w_gate2 + b_gate).astype(jnp.float32)) / GLA_TAU
    log_a = log_a.reshape(b_, l_, GLA_HEADS, GLA_DK)
    qc, kc, vc, lac = to_chunks(q), to_chunks(k), to_chunks(v), to_chunks(log_a)
    g = jnp.cumsum(lac, axis=2)
    g_end = g[:, :, -1]
    k_dec = kc * jnp.exp(g_end[:, :, None] - g)

    def body(state, inp):
        q_c, kd_c, v_c, ge_c = inp
        state = state * jnp.exp(ge_c)[..., None] + jnp.einsum('bshk,bshv->bhkv', kd_c, v_c)
        o = jnp.einsum('bqhk,bhkv->bqhv', q_c, state)
        return state, o

    s0 = jnp.zeros((b_, GLA_HEADS, GLA_DK, GLA_DV), jnp.float32)
    _, o = lax.scan(body, s0, (qc, k_dec, vc, g_end))
    o = rms_normalize(from_chunks(o)).reshape(b_, l_, hv) * norm_w
    o = o.astype(h.dtype) * jax.nn.silu(r)
    return o @ w_out


def pool_mixer(h, w_grp, b_grp, scale):
    b_, l_, d_ = h.shape
    hf = h.astype(jnp.float32)
    cs = jnp.concatenate([jnp.zeros((b_, 1, d_), jnp.float32), jnp.cumsum(hf, axis=1)], axis=1)
    t1 = jnp.arange(1, l_ + 1, dtype=jnp.float32)
    outs = []
    for gi, win in enumerate(POOL_WINDOWS):
        sl = slice(gi * POOL_GW, (gi + 1) * POOL_GW)
        c = cs[..., sl]
        lower = jnp.concatenate([jnp.zeros((b_, win - 1, POOL_GW), jnp.float32), c[:, :l_ + 1 - win]], axis=1)
        mean = (c[:, 1:] - lower) / jnp.minimum(t1, float(win))[None, :, None]
        dlt = (mean - hf[..., sl]).astype(h.dtype)
        outs.append(dlt @ w_grp[gi] + b_grp[gi])
    return jnp.concatenate(outs, axis=-1) * scale


def ssd_mixer(h, w_in, conv_w, conv_b, dt_bias, a_log, d_skip, norm_w, w_out):
    b_, l_, _ = h.shape
    G, HG, P, N = SSD_GROUPS, SSD_HPG, SSD_HEADDIM, SSD_STATE
    proj = h @ w_in
    z, xbc, dt = jnp.split(proj, [SSD_DINNER, SSD_DINNER + SSD_CONV_DIM], axis=-1)
    xbc = jax.nn.silu(causal_dwconv(xbc, conv_w, conv_b))
    xs, bm, cm = jnp.split(xbc, [SSD_DINNER, SSD_DINNER + G * N], axis=-1)
    xs = xs.reshape(b_, l_, G, HG, P)
    bm = bm.reshape(b_, l_, G, N)
    cm = cm.reshape(b_, l_, G, N)
    dt = jax.nn.softplus((dt + dt_bias).astype(jnp.float32)).reshape(b_, l_, G, HG)
    a_h = -jnp.exp(a_log.astype(jnp.float32)).reshape(G, HG)
    causal = jnp.tril(jnp.ones((CHUNK, CHUNK), bool))

    def body(state, inp):
        x_c, b_c, c_c, dt_c = inp
        acs = jnp.cumsum(dt_c * a_h, axis=1)
        seg = acs[:, :, None] - acs[:, None, :]
        lmat = jnp.exp(jnp.where(causal[None, :, :, None, None], seg, -jnp.inf))
        xdt = x_c * dt_c[..., None]
        cb = jnp.einsum('btgn,bsgn->btsg', c_c, b_c)
        y = jnp.einsum('btsg,btsgh,bsghp->btghp', cb, lmat, xdt)
        y = y + jnp.einsum('btgn,bghpn,btgh->btghp', c_c, state, jnp.exp(acs))
        decay_end = jnp.exp(acs[:, -1:] - acs)
        state = state * jnp.exp(acs[:, -1])[..., None, None] + jnp.einsum('bsgn,bsgh,bsghp->bghpn', b_c, decay_end, xdt)
        return state, y

    s0 = jnp.zeros((b_, G, HG, P, N), jnp.float32)
    _, ys = lax.scan(body, s0, (to_chunks(xs), to_chunks(bm), to_chunks(cm), to_chunks(dt)))
    y = from_chunks(ys) + d_skip.reshape(G, HG)[..., None] * xs
    y = y.reshape(b_, l_, SSD_DINNER) * jax.nn.silu(z)
    y = rms_normalize(y.reshape(b_, l_, G, SSD_DINNER // G)).reshape(b_, l_, SSD_DINNER) * norm_w
    return y.astype(h.dtype) @ w_out


def sb_mixer(h, w_qkv, w_out):
    b_, l_, _ = h.shape
    q, k, v = jnp.split(h @ w_qkv, 3, axis=-1)
    q = q.reshape(b_, l_, SB_HEADS, SB_HEADDIM).transpose(0, 2, 1, 3) * (SB_HEADDIM ** -0.5)
    k = k.reshape(b_, l_, SB_HEADS, SB_HEADDIM).transpose(0, 2, 1, 3)
    v = v.reshape(b_, l_, SB_HEADS, SB_HEADDIM).transpose(0, 2, 1, 3)
    nb = l_ // SB_QBLOCK
    qb = jnp.moveaxis(q.reshape(b_, SB_HEADS, nb, SB_QBLOCK, SB_HEADDIM), 2, 0)
    starts = jnp.arange(nb, dtype=jnp.int32) * SB_QBLOCK
    key_pos = jnp.arange(l_, dtype=jnp.int32)

    def block(inp):
        q_i, start = inp
        zt = jnp.einsum('bhqd,bhkd->bhqk', q_i, k).astype(jnp.float32)
        qpos = start + jnp.arange(SB_QBLOCK, dtype=jnp.int32)
        mask = key_pos[None, :] < qpos[:, None]
        log_1mb = jnp.where(mask, jax.nn.log_sigmoid(-zt), 0.0)
        rc = lax.cumsum(log_1mb, axis=3, reverse=True)
        surv = jnp.concatenate([rc[..., 1:], jnp.zeros_like(rc[..., :1])], axis=-1)
        att = jnp.where(mask, jnp.exp(jax.nn.log_sigmoid(zt) + surv), 0.0)
        return jnp.einsum('bhqk,bhkd->bhqd', att.astype(v.dtype), v)

    o = lax.map(block, (qb, starts))
    o = jnp.moveaxis(o, 0, 2).reshape(b_, SB_HEADS, l_, SB_HEADDIM)
    o = o.transpose(0, 2, 1, 3).reshape(b_, l_, D_MODEL)
    return o @ w_out


def conv_ffn(h, w_up, conv_w, conv_b, w_down):
    u = causal_dwconv(h @ w_up, conv_w, conv_b)
    g, val = jnp.split(u, 2, axis=-1)
    return (jax.nn.silu(g) * val) @ w_down


def setup_inputs(seed: int = 0) -> dict:
    key = jax.random.key(seed)
    ks = list(jax.random.split(key, 40))
    f32 = jnp.float32

    def nrm(i, shape, scale):
        return jax.random.normal(ks[i], shape, f32) * scale

    n_a = len(range(0, DEPTH, N_MIXERS))
    n_b = len(range(1, DEPTH, N_MIXERS))
    n_c = len(range(2, DEPTH, N_MIXERS))
    n_d = len(range(3, DEPTH, N_MIXERS))
    dt0 = jnp.exp(jax.random.uniform(ks[20], (n_c, SSD_HEADS), f32) * (math.log(0.1) - math.log(0.001)) + math.log(0.001))
    return {
        'x': nrm(0, (BATCH, SEQ, D_MODEL), 1.0),
        'mix_norm_w': 1.0 + nrm(1, (DEPTH, D_MODEL), 0.02),
        'ffn_norm_w': 1.0 + nrm(2, (DEPTH, D_MODEL), 0.02),
        'final_norm_w': 1.0 + nrm(3, (D_MODEL,), 0.02),
        'gla_w_in': nrm(4, (n_a, D_MODEL, GLA_IN), D_MODEL ** -0.5),
        'gla_w_gate2': nrm(5, (n_a, GLA_GATE_RANK, GLA_HEADS * GLA_DK), GLA_GATE_RANK ** -0.5),
        'gla_b_gate': nrm(6, (n_a, GLA_HEADS * GLA_DK), 0.1),
        'gla_norm_w': 1.0 + nrm(7, (n_a, GLA_HEADS * GLA_DV), 0.02),
        'gla_w_out': nrm(8, (n_a, GLA_HEADS * GLA_DV, D_MODEL), (GLA_HEADS * GLA_DV) ** -0.5),
        'pool_w': nrm(9, (n_b, POOL_GROUPS, POOL_GW, POOL_GW), POOL_GW ** -0.5),
        'pool_b': nrm(10, (n_b, POOL_GROUPS, POOL_GW), 0.02),
        'pool_scale': 1.0 + nrm(11, (n_b, D_MODEL), 0.02),
        'ssd_w_in': nrm(12, (n_c, D_MODEL, SSD_IN), D_MODEL ** -0.5),
        'ssd_conv_w': nrm(13, (n_c, SSD_CONV, SSD_CONV_DIM), SSD_CONV ** -0.5),
        'ssd_conv_b': nrm(14, (n_c, SSD_CONV_DIM), 0.02),
        'ssd_dt_bias': dt0 + jnp.log(-jnp.expm1(-dt0)),
        'ssd_a_log': jnp.log(jax.random.uniform(ks[15], (n_c, SSD_HEADS), f32, 1.0, 16.0)),
        'ssd_d': 1.0 + nrm(16, (n_c, SSD_HEADS), 0.02),
        'ssd_norm_w': 1.0 + nrm(17, (n_c, SSD_DINNER), 0.02),
        'ssd_w_out': nrm(18, (n_c, SSD_DINNER, D_MODEL), SSD_DINNER ** -0.5),
        'sb_w_qkv': nrm(19, (n_d, D_MODEL, 3 * D_MODEL), D_MODEL ** -0.5),
        'sb_w_out': nrm(21, (n_d, D_MODEL, D_MODEL), D_MODEL ** -0.5),
        'ffn_w_up': nrm(22, (DEPTH, D_MODEL, 2 * FFN_DIM), D_MODEL ** -0.5),
        'ffn_conv_w': nrm(23, (DEPTH, FFN_CONV, 2 * FFN_DIM), FFN_CONV ** -0.5),
        'ffn_conv_b': nrm(24, (DEPTH, 2 * FFN_DIM), 0.02),
        'ffn_w_down': nrm(25, (DEPTH, FFN_DIM, D_MODEL), FFN_DIM ** -0.5),
    }


def reference(x, mix_norm_w, ffn_norm_w, final_norm_w,
              gla_w_in, gla_w_gate2, gla_b_gate, gla_norm_w, gla_w_out,
              pool_w, pool_b, pool_scale,
              ssd_w_in, ssd_conv_w, ssd_conv_b, ssd_dt_bias, ssd_a_log, ssd_d, ssd_norm_w, ssd_w_out,
              sb_w_qkv, sb_w_out,
              ffn_w_up, ffn_conv_w, ffn_conv_b, ffn_w_down):
    for i in range(DEPTH):
        m, j = i % N_MIXERS, i // N_MIXERS
        h = rmsnorm(x, mix_norm_w[i])
        if m == 0:
            y = gla_mixer(h, gla_w_in[j], gla_w_gate2[j], gla_b_gate[j], gla_norm_w[j], gla_w_out[j])
        elif m == 1:
            y = pool_mixer(h, pool_w[j], pool_b[j], pool_scale[j])
        elif m == 2:
            y = ssd_mixer(h, ssd_w_in[j], ssd_conv_w[j], ssd_conv_b[j], ssd_dt_bias[j], ssd_a_log[j],
                          ssd_d[j], ssd_norm_w[j], ssd_w_out[j])
        else:
            y = sb_mixer(h, sb_w_qkv[j], sb_w_out[j])
        x = x + y.astype(x.dtype)
        h = rmsnorm(x, ffn_norm_w[i])
        x = x + conv_ffn(h, ffn_w_up[i], ffn_conv_w[i], ffn_conv_b[i], ffn_w_down[i]).astype(x.dtype)
    return rmsnorm(x, final_norm_w)
```

```python
import numpy as np
import ml_dtypes
from contextlib import ExitStack
import concourse.bass as bass
import concourse.mybir as mybir
from concourse.bass_utils import run_bass_kernel_spmd

F32 = mybir.dt.float32
BF16 = mybir.dt.bfloat16
ALU = mybir.AluOpType
AF = mybir.ActivationFunctionType
NPBF = ml_dtypes.bfloat16

D = 1024
L = 4096
EPS = 1e-6
FF = 2816
TS = 384
SK_ = (3, 1)


class Buf:
    __slots__ = ("name", "last_w", "readers")

    def __init__(self, name=""):
        self.name = name
        self.last_w = None
        self.readers = []


class Op:
    __slots__ = ("idx", "eng", "fn", "deps", "dma", "seq", "signal", "count", "sem_i", "sem_val", "prewait")

    def __init__(self, idx, eng, fn, dma):
        self.idx = idx
        self.eng = eng
        self.fn = fn
        self.dma = dma
        self.deps = set()
        self.seq = None
        self.signal = False
        self.count = None
        self.sem_i = None
        self.sem_val = None
        self.prewait = None


ENGS = ["pe", "act", "dve", "pool", "sp"]
N_DMA_SEMS = 12


class Prog:
    def __init__(self, nc):
        self.nc = nc
        self.ops = []
        self.eng_ops = {e: [] for e in ENGS}
        self.barrier_set = None
        self.barrier_pending = set()
        self.dma_ops_since_barrier = []
        self._cap = None

    def capture(self):
        self._cap = []

    def end_capture(self):
        c = self._cap
        self._cap = None
        return c

    def emit_interleaved(self, lists, weights=None):
        idx = [0] * len(lists)
        tot = [max(1, len(l)) for l in lists]
        if weights is not None:
            tot = [t * w for t, w in zip(tot, weights)]
        while True:
            best = None
            for i, l in enumerate(lists):
                if idx[i] < len(l):
                    f = idx[i] / tot[i]
                    if best is None or f < best[0]:
                        best = (f, i)
            if best is None:
                break
            i = best[1]
            self.op(*lists[i][idx[i]])
            idx[i] += 1

    def op(self, eng, fn, reads=(), writes=(), dma=False):
        if self._cap is not None:
            self._cap.append((eng, fn, list(reads), list(writes), dma))
            return None
        o = Op(len(self.ops), eng, fn, dma)
        for b in reads:
            if b.last_w is not None:
                o.deps.add(b.last_w)
        for b in writes:
            if b.last_w is not None:
                o.deps.add(b.last_w)
            o.deps.update(b.readers)
        for b in reads:
            b.readers.append(o.idx)
        for b in writes:
            b.last_w = o.idx
            b.readers = []
        if eng in self.barrier_pending:
            o.deps.update(self.barrier_set)
            self.barrier_pending.discard(eng)
        o.seq = len(self.eng_ops[eng])
        self.eng_ops[eng].append(o)
        self.ops.append(o)
        if dma:
            self.dma_ops_since_barrier.append(o.idx)
        return o

    def barrier(self):
        s = set(self.dma_ops_since_barrier)
        for e in ENGS:
            if self.eng_ops[e]:
                s.add(self.eng_ops[e][-1].idx)
        self.barrier_set = s
        self.barrier_pending = set(ENGS)
        self.dma_ops_since_barrier = []

    def emit(self):
        nc = self.nc
        ops = self.ops
        dma_cnt = {e: [0] * N_DMA_SEMS for e in ENGS}
        dma_rr = {e: 0 for e in ENGS}
        last_on_sem = {e: [None] * N_DMA_SEMS for e in ENGS}
        for o in ops:
            if o.dma:
                i = dma_rr[o.eng] % N_DMA_SEMS
                dma_rr[o.eng] += 1
                o.sem_i = i
                o.prewait = last_on_sem[o.eng][i]
                dma_cnt[o.eng][i] += 1
                o.sem_val = 16 * dma_cnt[o.eng][i]
                last_on_sem[o.eng][i] = o.idx
        known = {e: {} for e in ENGS}
        waits = {}
        for o in ops:
            need = {}
            deps = set(o.deps)
            if o.prewait is not None:
                deps.add(o.prewait)
            for d in deps:
                do = ops[d]
                if do.dma:
                    key = ("dma", do.eng, do.sem_i)
                    if need.get(key, 0) < do.sem_val:
                        need[key] = do.sem_val
                else:
                    if do.eng == o.eng:
                        if o.eng in ("pe", "sp") or o.dma:
                            continue
                    key = ("eng", do.eng)
                    if key not in need or ops[need[key]].seq < do.seq:
                        need[key] = d
            wl = []
            for key, v in need.items():
                if key[0] == "dma":
                    if known[o.eng].get(key, 0) >= v:
                        continue
                    known[o.eng][key] = v
                    wl.append((key, v))
                else:
                    do = ops[v]
                    if known[o.eng].get(key, -1) >= do.seq:
                        continue
                    known[o.eng][key] = do.seq
                    do.signal = True
                    wl.append((key, v))
            waits[o.idx] = wl
        for e in ENGS:
            c = 0
            for o in self.eng_ops[e]:
                if not o.dma and o.signal:
                    c += 1
                    o.count = c
        with ExitStack() as st:
            esem = {e: st.enter_context(nc.semaphore("es_" + e)) for e in ENGS}
            dsem = {e: [st.enter_context(nc.semaphore("ds_%s_%d" % (e, i))) for i in range(N_DMA_SEMS)]
                    for e in ENGS if dma_rr[e] > 0}
            block = st.enter_context(nc.Block())

            def run(e, eng):
                for o in self.eng_ops[e]:
                    for key, v in waits[o.idx]:
                        if key[0] == "dma":
                            eng.wait_ge(dsem[key[1]][key[2]], v)
                        else:
                            eng.wait_ge(esem[key[1]], ops[v].count)
                    ins = o.fn(eng)
                    if o.dma:
                        ins.then_inc(dsem[e][o.sem_i], 16)
                    elif o.signal:
                        ins.then_inc(esem[e], 1)
                if e == "sp":
                    for ee in dsem:
                        for i in range(N_DMA_SEMS):
                            if dma_cnt[ee][i] > 0:
                                eng.wait_ge(dsem[ee][i], 16 * dma_cnt[ee][i])

            block.tensor(lambda eng: run("pe", eng))
            block.scalar(lambda eng: run("act", eng))
            block.vector(lambda eng: run("dve", eng))
            block.gpsimd(lambda eng: run("pool", eng))
            block.sync(lambda eng: run("sp", eng))


class Ctx:
    pass


def MM(C, out, lhsT, rhs, start, stop, reads, writes):
    C.P.op("pe", lambda e: e.matmul(out, lhsT=lhsT, rhs=rhs, start=start, stop=stop), reads, writes)


def TR(C, out, in_, reads, writes):
    C.P.op("pe", lambda e: e.transpose(out=out, in_=in_, identity=C.ident[:]), list(reads) + [C.Bconst], writes)


def ACT(C, out, in_, func, reads, writes, **kw):
    C.P.op("act", lambda e: e.activation(out=out, in_=in_, func=func, **kw), reads, writes)


def TT(C, eng, out, in0, in1, op, reads, writes):
    C.P.op(eng, lambda e: e.tensor_tensor(out=out, in0=in0, in1=in1, op=op), reads, writes)


def STT(C, out, in0, scalar, in1, op0, op1, reads, writes):
    C.P.op("dve", lambda e: e.scalar_tensor_tensor(out=out, in0=in0, scalar=scalar, in1=in1, op0=op0, op1=op1), reads, writes)


def TSC(C, eng, out, in0, s1, s2, op0, op1, reads, writes):
    C.P.op(eng, lambda e: e.tensor_scalar(out=out, in0=in0, scalar1=s1, scalar2=s2, op0=op0, op1=op1), reads, writes)


def CP(C, eng, out, in_, reads, writes):
    C.P.op(eng, lambda e: e.tensor_copy(out=out, in_=in_), reads, writes)


def MSET(C, eng, ap, val, writes):
    C.P.op(eng, lambda e: e.memset(ap, val), (), writes)


def DMA(C, out, in_, reads, writes, eng="sp"):
    C.P.op(eng, lambda e: e.dma_start(out=out, in_=in_), reads, writes, dma=True)


def load_w(C, st, name, dram_ap, ncols, dt=BF16, piece=8192):
    t = st.enter_context(C.nc.sbuf_tensor(C.un(name), [128, ncols], dt))
    bufs = []
    c0 = 0
    while c0 < ncols:
        c1 = min(ncols, c0 + piece)
        b = Buf(name)
        DMA(C, t[:, c0:c1], dram_ap[:, c0:c1], (), [b], eng=("pool" if dt == BF16 else "sp"))
        bufs.append(b)
        c0 = c1
    return t, bufs


def load_rows(C, st, name, dram_ap, nrows, ncols, dt=F32):
    t = st.enter_context(C.nc.sbuf_tensor(C.un(name), [nrows, ncols], dt))
    b = Buf(name)
    DMA(C, t[:], dram_ap, (), [b], eng=("pool" if dt == BF16 else "sp"))
    return t, b


def rms_rstd(C, xsrc, W, n_feat, nk, sqb, Bsq, rstd, Brstd, Bx):
    b = C.bank()
    for k in range(nk):
        s = k % 2
        ACT(C, sqb[s][:, :W], xsrc(k), AF.Square, [Bx], [Bsq[s]])
        MM(C, C.ps[b][:, :W], C.ones[:], sqb[s][:, :W], k == 0, k == nk - 1, [Bsq[s], C.Bconst], [C.PS[b]])
    ACT(C, rstd[:, :W], C.ps[b][:, :W], AF.Ln, [C.PS[b], C.Bconst], [Brstd], scale=1.0 / n_feat, bias=C.epsc[:, 0:1])
    ACT(C, rstd[:, :W], rstd[:, :W], AF.Exp, [Brstd], [Brstd], scale=-0.5)


def norm_tile(C, xt, W, nw, hT, sqb, Bsq, rstd, Brstd, Bx, Bh):
    rms_rstd(C, lambda k: xt[:, k, :W], W, D, 8, sqb, Bsq, rstd, Brstd, Bx)
    for k in range(8):
        STT(C, hT[:, k, :W], xt[:, k, :W], nw[:, k:k + 1], rstd[:, :W], ALU.mult, ALU.mult, [Bx, Brstd, C.Bconst], [Bh])


def tiles(stride):
    out = []
    t0 = 0
    while t0 < L:
        out.append((t0, min(stride, L - t0)))
        t0 += stride
    return out


def ffn_stage(C, layer, src, dst, final=False):
    nc, P = C.nc, C.P
    HL = 2
    WMAX = TS + HL
    with ExitStack() as st:
        sb = lambda name, shape, dt: st.enter_context(nc.sbuf_tensor(C.un(name), shape, dt))
        wup_t, Bwup = load_w(C, st, "wup", C.din["ffn_wup"][layer], 44 * 8 * 128, piece=4096)
        wdn_t, Bwdn = load_w(C, st, "wdn", C.din["ffn_wdn"][layer], 22 * 1024, piece=4096)
        wup = wup_t[:].rearrange("p (m k c) -> p m k c", m=44, k=8)
        wdn = wdn_t[:].rearrange("p (k c) -> p k c", k=22)
        xts = [sb("xt%d" % i, [128, 8, WMAX], F32) for i in range(2)]
        Bxt = [Buf("xt") for _ in range(2)]
        hTs = [sb("hT%d" % i, [128, 8, WMAX], BF16) for i in range(2)]
        Bhs = [Buf("hT") for _ in range(2)]
        sqb = [sb("sq%d" % i, [128, WMAX], BF16) for i in range(2)]
        Bsq = [Buf("sq") for _ in range(2)]
        rstd = sb("rstd", [128, WMAX], F32)
        Brstd = Buf("rstd")
        A = sb("A", [128, 22, TS], BF16)
        BA = [Buf("A") for _ in range(22)]
        U = [sb("U%d" % i, [128, TS], F32) for i in range(4)]
        BU = [Buf("U") for _ in range(4)]
        S = [sb("S%d" % i, [128, TS], F32) for i in range(2)]
        BS = [Buf("S") for _ in range(2)]
        cw = C.ffn_cw
        cb = C.ffn_cb
        nw = C.ffn_nw[:, layer, :]
        tl = tiles(TS)

        def load_norm(ti):
            t0, T = tl[ti]
            xt, Bx = xts[ti % 2], Bxt[ti % 2]
            W = T + HL
            if ti == 0:
                MSET(C, "pool", xt[:, :, 0:HL], 0.0, [Bx])
                DMA(C, xt[:, :, HL:W], src[:, :, 0:T], (), [Bx])
            else:
                DMA(C, xt[:, :, 0:W], src[:, :, t0 - HL:t0 + T], (), [Bx])
            norm_tile(C, xt, W, nw, hTs[ti % 2], sqb, Bsq, rstd, Brstd, Bx, Bhs[ti % 2])

        load_norm(0)
        for ti, (t0, T) in enumerate(tl):
            xt, Bx = xts[ti % 2], Bxt[ti % 2]
            hT, Bh = hTs[ti % 2], Bhs[ti % 2]
            W = T + HL
            for j in range(22):
                bs_, uis = [], []
                for half in range(2):
                    m = j + 22 * half
                    b = C.bank()
                    for k in range(8):
                        MM(C, C.ps[b][:, :W], wup[:, m, k, :], hT[:, k, :W], k == 0, k == 7, [Bh, Bwup[m // 4]], [C.PS[b]])
                    bs_.append(b)
                    uis.append((2 * j + half) % 4)
                for half in range(2):
                    m, b, ui = j + 22 * half, bs_[half], uis[half]
                    ACT(C, U[ui][:, :T], C.ps[b][:, 2:W], AF.Identity, [C.PS[b], C.Bconst], [BU[ui]],
                        scale=cw[:, layer, 2, m:m + 1], bias=cb[:, layer, m:m + 1])
                for tap in (1, 0):
                    for half in range(2):
                        m, b, ui = j + 22 * half, bs_[half], uis[half]
                        STT(C, U[ui][:, :T], C.ps[b][:, tap:W - 2 + tap], cw[:, layer, tap, m:m + 1], U[ui][:, :T], ALU.mult, ALU.add,
                            [C.PS[b], C.Bconst, BU[ui]], [BU[ui]])
                si = j % 2
                ACT(C, S[si][:, :T], U[uis[0]][:, :T], AF.Silu, [BU[uis[0]]], [BS[si]])
                TT(C, "pool", A[:, j, :T], S[si][:, :T], U[uis[1]][:, :T], ALU.mult, [BS[si], BU[uis[1]]], [BA[j]])
            for m in range(8):
                if m == 3 and ti + 1 < len(tl):
                    load_norm(ti + 1)
                b = C.bank()
                for k in range(22):
                    MM(C, C.ps[b][:, :T], wdn[:, k, m * 128:(m + 1) * 128], A[:, k, :T], k == 0, k == 21, [BA[k], Bwdn[k // 4]], [C.PS[b]])
                TT(C, "dve", xt[:, m, HL:W], xt[:, m, HL:W], C.ps[b][:, :T], ALU.add, [C.PS[b], Bx], [Bx])
            if final:
                rms_rstd(C, lambda k: xt[:, k, HL:W], T, D, 8, sqb, Bsq, rstd, Brstd, Bx)
                for k in range(8):
                    STT(C, xt[:, k, HL:W], xt[:, k, HL:W], C.fin_nw[:, k:k + 1], rstd[:, :T], ALU.mult, ALU.mult,
                        [Bx, Brstd, C.Bconst], [Bx])
            DMA(C, dst[:, :, t0:t0 + T], xt[:, :, HL:W], [Bx], [])
    P.barrier()


def final_norm_stage(C, src, dst):
    nc, P = C.nc, C.P
    with ExitStack() as st:
        sb = lambda name, shape, dt: st.enter_context(nc.sbuf_tensor(C.un(name), shape, dt))
        xts = [sb("xt%d" % i, [128, 8, 512], F32) for i in range(2)]
        Bxt = [Buf("xt") for _ in range(2)]
        yts = [sb("yt%d" % i, [128, 8, 512], F32) for i in range(2)]
        Byt = [Buf("yt") for _ in range(2)]
        sqb = [sb("sq%d" % i, [128, 512], BF16) for i in range(2)]
        Bsq = [Buf("sq") for _ in range(2)]
        rstd = sb("rstd", [128, 512], F32)
        Brstd = Buf("rstd")
        for ti, (t0, T) in enumerate(tiles(512)):
            xt, Bx = xts[ti % 2], Bxt[ti % 2]
            yt, By = yts[ti % 2], Byt[ti % 2]
            DMA(C, xt[:, :, :T], src[:, :, t0:t0 + T], (), [Bx])
            norm_tile(C, xt, T, C.fin_nw, yt, sqb, Bsq, rstd, Brstd, Bx, By)
            DMA(C, dst[:, :, t0:t0 + T], yt[:, :, :T], [By], [])
    P.barrier()


def pool_stage(C, src, dst):
    nc, P = C.nc, C.P
    HL = 15
    WMAX = TS + HL
    with ExitStack() as st:
        sb = lambda name, shape, dt: st.enter_context(nc.sbuf_tensor(C.un(name), shape, dt))
        wp_t, Bwp = load_w(C, st, "wp", C.din["pool_w"], 4 * 2 * 256)
        wp = wp_t[:].rearrange("p (g k c) -> p g k c", g=4, k=2)
        invc, Binv = load_w(C, st, "invc", C.din["pool_inv"], 4 * WMAX, dt=F32)
        invc = invc[:].rearrange("p (g w) -> p g w", g=4)
        xts = [sb("xt%d" % i, [128, 8, WMAX], F32) for i in range(2)]
        Bxt = [Buf("xt") for _ in range(2)]
        hT_, Bh_ = [sb("hT%d" % i, [128, 8, WMAX], F32) for i in range(2)], [Buf("hT") for _ in range(2)]
        wa_, Bwa_ = [sb("wa%d" % i, [128, 2, WMAX], F32) for i in range(2)], [Buf("wa") for _ in range(2)]
        wb_, Bwb_ = [sb("wb%d" % i, [128, 2, WMAX], F32) for i in range(2)], [Buf("wb") for _ in range(2)]
        dl_, Bdl_ = [sb("dl%d" % i, [128, 8, WMAX], BF16) for i in range(2)], [Buf("dl") for _ in range(2)]
        sqb_ = [[sb("sq%d_%d" % (i, j), [128, WMAX], BF16) for i in range(2)] for j in range(2)]
        Bsq_ = [[Buf("sq") for _ in range(2)] for j in range(2)]
        rstd_, Brstd_ = [sb("rstd%d" % i, [128, WMAX], F32) for i in range(2)], [Buf("rstd") for _ in range(2)]
        tmp_, Btmp_ = [sb("tmp%d" % i, [128, TS], F32) for i in range(2)], [Buf("tmp") for _ in range(2)]
        nw = C.mix_nw[:, 1, :]

        def do_tile(ti, t0, T):
            xt, Bx = xts[ti % 2], Bxt[ti % 2]
            hT, Bh = hT_[ti % 2], Bh_[ti % 2]
            wa, Bwa, wb, Bwb = wa_[ti % 2], Bwa_[ti % 2], wb_[ti % 2], Bwb_[ti % 2]
            dl, Bdl = dl_[ti % 2], Bdl_[ti % 2]
            sqb, Bsq = sqb_[ti % 2], Bsq_[ti % 2]
            rstd, Brstd = rstd_[ti % 2], Brstd_[ti % 2]
            tmp, Btmp = tmp_[ti % 2], Btmp_[ti % 2]
            W = T + HL
            if ti == 0:
                MSET(C, "pool", xt[:, :, 0:HL], 0.0, [Bx])
                DMA(C, xt[:, :, HL:W], src[:, :, 0:T], (), [Bx])
            else:
                DMA(C, xt[:, :, 0:W], src[:, :, t0 - HL:t0 + T], (), [Bx])
            norm_tile(C, xt, W, nw, hT, sqb, Bsq, rstd, Brstd, Bx, Bh)
            for g in range(4):
                win = 2 << g
                cur, Bcur = hT[:, 2 * g:2 * g + 2, :], Bh
                bufs = [(wa, Bwa), (wb, Bwb)]
                sh = 1
                i = 0
                while sh < win:
                    nxt, Bn = bufs[i % 2]
                    TT(C, "dve", nxt[:, :, sh:W], cur[:, :, sh:W], cur[:, :, 0:W - sh], ALU.add, [Bcur], [Bn])
                    if sh > 1 or True:
                        CP(C, "pool", nxt[:, :, 0:sh], cur[:, :, 0:sh], [Bcur], [Bn])
                    cur, Bcur = nxt[:, :, :], Bn
                    sh *= 2
                    i += 1
                if ti == 0:
                    TT(C, "dve", cur[:, :, :W], cur[:, :, :W], invc[:, g:g + 1, :W].to_broadcast([128, 2, W]), ALU.mult,
                       [Bcur] + Binv, [Bcur])
                    TT(C, "dve", dl[:, 2 * g:2 * g + 2, :W], cur[:, :, :W], hT[:, 2 * g:2 * g + 2, :W], ALU.subtract,
                       [Bcur, Bh], [Bdl])
                else:
                    C.P.op("dve", (lambda e, o=dl[:, 2 * g:2 * g + 2, :W], a=cur[:, :, :W], bb=hT[:, 2 * g:2 * g + 2, :W], s=1.0 / win:
                                   e.scalar_tensor_tensor(out=o, in0=a, scalar=s, in1=bb, op0=ALU.mult, op1=ALU.subtract)),
                           [Bcur, Bh], [Bdl])
            for g in range(4):
                for j in range(2):
                    c = 2 * g + j
                    b = C.bank()
                    for k in range(2):
                        MM(C, C.ps[b][:, :T], wp[:, g, k, j * 128:(j + 1) * 128], dl[:, 2 * g + k, HL:W], k == 0, k == 1,
                           [Bdl] + Bwp, [C.PS[b]])
                    TSC(C, "dve", tmp[:, :T], C.ps[b][:, :T], C.pool_b[:, c:c + 1], C.pool_s[:, c:c + 1], ALU.add, ALU.mult,
                        [C.PS[b], C.Bconst], [Btmp])
                    TT(C, "dve", xt[:, c, HL:W], xt[:, c, HL:W], tmp[:, :T], ALU.add, [Btmp, Bx], [Bx])
            DMA(C, dst[:, :, t0:t0 + T], xt[:, :, HL:W], [Bx], [])

        tl = tiles(TS)
        for j in range(0, len(tl), 2):
            lists = []
            for jj in range(j, min(j + 2, len(tl))):
                C.bankset = [0, 1, 2, 3] if jj % 2 == 0 else [4, 5, 6, 7]
                P.capture()
                do_tile(jj, tl[jj][0], tl[jj][1])
                lists.append(P.end_capture())
            C.bankset = list(range(8))
            P.emit_interleaved(lists)
    P.barrier()


def gla_stage(C, src, dst):
    nc, P = C.nc, C.P
    NT = 128
    with ExitStack() as st:
        sb = lambda name, shape, dt: st.enter_context(nc.sbuf_tensor(C.un(name), shape, dt))
        wq_t, Bwq = load_w(C, st, "wq", C.din["gla_wq"], 4 * 8 * 128)
        wk_t, Bwk = load_w(C, st, "wk", C.din["gla_wk"], 8 * 512)
        wv_t, Bwv = load_w(C, st, "wv", C.din["gla_wv"], 2 * 8 * 512)
        wr_t, Bwr = load_w(C, st, "wr", C.din["gla_wr"], 8 * 8 * 128)
        wg_t, Bwg = load_w(C, st, "wg", C.din["gla_wg"], 8 * 16)
        wo_t, Bwo = load_w(C, st, "wo", C.din["gla_wo"], 8 * 1024)
        wq = wq_t[:].rearrange("p (m k c) -> p m k c", m=4, k=8)
        wk = wk_t[:].rearrange("p (k c) -> p k c", k=8)
        wv = wv_t[:].rearrange("p (m k c) -> p m k c", m=2, k=8)
        wr = wr_t[:].rearrange("p (m k c) -> p m k c", m=8, k=8)
        wg = wg_t[:].rearrange("p (k c) -> p k c", k=8)
        wo = wo_t[:].rearrange("p (k c) -> p k c", k=8)
        g2, Bg2 = load_rows(C, st, "g2", C.din["gla_g2"], 17, 512, BF16)
        m1, Bm1 = load_rows(C, st, "m1", C.din["gla_m1"], 128, 128, F32)
        ncol, Bncol = load_rows(C, st, "ncol", C.din["gla_ncol"], 128, 1, F32)
        xts = [sb("xt%d" % i, [128, 8, NT], F32) for i in range(2)]
        Bxt = [Buf("xt") for _ in range(2)]
        hT = sb("hT", [128, 8, NT], BF16)
        Bh = Buf("hT")
        sqb = [sb("sq%d" % i, [128, 1024], BF16) for i in range(2)]
        Bsq = [Buf("sq") for _ in range(2)]
        rstd = sb("rstd", [128, 512], F32)
        Brstd = Buf("rstd")
        qTs = [sb("qT%d" % i, [128, 4, NT], BF16) for i in range(2)]
        BqTs = [Buf("qT") for _ in range(2)]
        glr = sb("glr", [17, NT], BF16)
        Bglr = Buf("glr")
        E = sb("E", [128, 512], F32)
        BE = Buf("E")
        Lt = sb("Lt", [128, 512], F32)
        BL = Buf("L")
        ED = sb("ED", [128, 512], F32)
        BED = Buf("ED")
        kdecs = [sb("kdec%d" % i, [128, 512], BF16) for i in range(2)]
        Bkds = [Buf("kdec") for _ in range(2)]
        vbs = [sb("vb%d" % i, [128, 1024], BF16) for i in range(2)]
        Bvbs = [Buf("vb") for _ in range(2)]
        egs = [sb("eg%d" % i, [128, 8], F32) for i in range(2)]
        Begs = [Buf("eg") for _ in range(2)]
        sqb2 = [sb("sqo%d" % i, [128, 512], BF16) for i in range(2)]
        Bsq2 = [Buf("sqo") for _ in range(2)]
        rstd2 = sb("rstd2", [128, 512], F32)
        Brstd2 = Buf("rstd2")
        Sst = sb("Sst", [128, 4, 256], F32)
        BSst = [Buf("S") for _ in range(4)]
        Sbf = sb("Sbf", [128, 4, 256], BF16)
        BSbf = [Buf("Sbf") for _ in range(4)]
        silrs = [sb("silr%d" % i, [128, 8, NT], F32) for i in range(2)]
        Bsilrs = [Buf("silr") for _ in range(2)]
        t1 = sb("t1", [128, 8, NT], F32)
        Bt1 = Buf("t1")
        og = sb("og", [128, 8, NT], BF16)
        Bog = Buf("og")
        MSET(C, "pool", glr[:], 1.0, [Bglr])
        MSET(C, "pool", Sst[:], 0.0, BSst)
        nw = C.mix_nw[:, 0, :]
        def p1(ti):
            t0 = ti * NT
            xt, Bx = xts[ti % 2], Bxt[ti % 2]
            qT, BqT = qTs[ti % 2], BqTs[ti % 2]
            kdec, Bkd = kdecs[ti % 2], Bkds[ti % 2]
            vb, Bvb = vbs[ti % 2], Bvbs[ti % 2]
            eg, Beg = egs[ti % 2], Begs[ti % 2]
            silr, Bsilr = silrs[ti % 2], Bsilrs[ti % 2]
            DMA(C, xt[:], src[:, :, t0:t0 + NT], (), [Bx])
            norm_tile(C, xt, NT, nw, hT, sqb, Bsq, rstd, Brstd, Bx, Bh)
            bq = C.bank()
            for m in range(4):
                for k in range(8):
                    MM(C, C.ps[bq][:, m * NT:(m + 1) * NT], wq[:, m, k, :], hT[:, k, :], k == 0, k == 7, [Bh] + Bwq, [C.PS[bq]])
            ACT(C, qT[:].rearrange("p m t -> p (m t)"), C.ps[bq][:, :], AF.Identity, [C.PS[bq]], [BqT], scale=128.0 ** -0.5)
            bg = C.bank()
            for k in range(8):
                MM(C, C.ps[bg][0:16, :NT], wg[:, k, :], hT[:, k, :], k == 0, k == 7, [Bh] + Bwg, [C.PS[bg]])
            ACT(C, glr[0:16, :], C.ps[bg][0:16, :NT], AF.Copy, [C.PS[bg]], [Bglr])
            bz = C.bank()
            MM(C, C.ps[bz][:, :], glr[:, :], g2[:, :], True, True, [Bglr, Bg2], [C.PS[bz]])
            ACT(C, E[:], C.ps[bz][:, :], AF.Exp, [C.PS[bz]], [BE], scale=-1.0)
            ACT(C, Lt[:], E[:], AF.Ln, [BE, C.Bconst], [BL], bias=C.onec[:, 0:1])
            bd = C.bank()
            MM(C, C.ps[bd][:, :], m1[:], Lt[:], True, True, [BL, Bm1], [C.PS[bd]])
            ACT(C, ED[:], C.ps[bd][:, :], AF.Exp, [C.PS[bd]], [BED])
            be = C.bank()
            for h in range(4):
                for c in range(2):
                    MM(C, C.ps[be][:, h * 2 + c:h * 2 + c + 1], Lt[64 * c:64 * c + 64, h * 128:(h + 1) * 128],
                       ncol[64 * c:64 * c + 64, :], True, True, [BL, Bncol], [C.PS[be]])
            ACT(C, eg[:], C.ps[be][:, 0:8], AF.Exp, [C.PS[be]], [Beg])
            bk = C.bank()
            for k in range(8):
                MM(C, C.ps[bk][:, :], hT[:, k, :], wk[:, k, :], k == 0, k == 7, [Bh] + Bwk, [C.PS[bk]])
            TT(C, "dve", kdec[:], C.ps[bk][:, :], ED[:], ALU.mult, [C.PS[bk], BED], [Bkd])
            for m in range(2):
                bv = C.bank()
                for k in range(8):
                    MM(C, C.ps[bv][:, :], hT[:, k, :], wv[:, m, k, :], k == 0, k == 7, [Bh] + Bwv, [C.PS[bv]])
                ACT(C, vb[:, m * 512:(m + 1) * 512], C.ps[bv][:, :], AF.Copy, [C.PS[bv]], [Bvb])
            for half in range(2):
                br = C.bank()
                for mm_ in range(4):
                    m = half * 4 + mm_
                    for k in range(8):
                        MM(C, C.ps[br][:, mm_ * NT:(mm_ + 1) * NT], wr[:, m, k, :], hT[:, k, :], k == 0, k == 7, [Bh] + Bwr, [C.PS[br]])
                ACT(C, silr[:, half * 4:half * 4 + 4, :].rearrange("p m t -> p (m t)"), C.ps[br][:, :], AF.Silu, [C.PS[br]], [Bsilr])
        def p2(ti):
            t0 = ti * NT
            xt, Bx = xts[ti % 2], Bxt[ti % 2]
            qT, BqT = qTs[ti % 2], BqTs[ti % 2]
            kdec, Bkd = kdecs[ti % 2], Bkds[ti % 2]
            vb, Bvb = vbs[ti % 2], Bvbs[ti % 2]
            eg, Beg = egs[ti % 2], Begs[ti % 2]
            silr, Bsilr = silrs[ti % 2], Bsilrs[ti % 2]
            bo = [C.hold(), C.hold()]
            for c in range(2):
                bus = [C.bank() for _ in range(4)]
                for h in range(4):
                    MM(C, C.ps[bus[h]][:, 0:256], kdec[64 * c:64 * c + 64, h * 128:(h + 1) * 128],
                       vb[64 * c:64 * c + 64, h * 256:(h + 1) * 256], True, True, [Bkd, Bvb], [C.PS[bus[h]]])
                for h in range(4):
                    STT(C, Sst[:, h, :], Sst[:, h, :], eg[:, 2 * h + c:2 * h + c + 1], C.ps[bus[h]][:, 0:256], ALU.mult, ALU.add,
                        [C.PS[bus[h]], Beg, BSst[h]], [BSst[h]])
                for h in range(4):
                    if h % 2 == 0:
                        CP(C, "pool", Sbf[:, h, :], Sst[:, h, :], [BSst[h]], [BSbf[h]])
                    else:
                        ACT(C, Sbf[:, h, :], Sst[:, h, :], AF.Copy, [BSst[h]], [BSbf[h]])
                for h in range(4):
                    for j in range(2):
                        c8 = 2 * h + j
                        MM(C, C.ps[bo[c8 // 4]][:, (c8 % 4) * NT + 64 * c:(c8 % 4) * NT + 64 * c + 64],
                           Sbf[:, h, j * 128:(j + 1) * 128], qT[:, h, 64 * c:64 * c + 64], True, True,
                           [BSbf[h], BqT], [C.PS[bo[c8 // 4]]])
            bs = C.bank()
            for half in range(2):
                ACT(C, sqb2[half][:, 0:4 * NT], C.ps[bo[half]][:, :], AF.Square, [C.PS[bo[half]]], [Bsq2[half]])
            for h in range(4):
                for j in range(2):
                    c8 = 2 * h + j
                    MM(C, C.ps[bs][:, h * NT:(h + 1) * NT], C.ones[:], sqb2[c8 // 4][:, (c8 % 4) * NT:(c8 % 4 + 1) * NT],
                       j == 0, j == 1, [Bsq2[c8 // 4], C.Bconst], [C.PS[bs]])
            ACT(C, rstd2[:, :], C.ps[bs][:, :], AF.Ln, [C.PS[bs], C.Bconst], [Brstd2], scale=1.0 / 256, bias=C.epsc[:, 0:1])
            ACT(C, rstd2[:, :], rstd2[:, :], AF.Exp, [Brstd2], [Brstd2], scale=-0.5)
            rs4 = rstd2[:].rearrange("p (h t) -> p h t", h=4)
            for half in range(2):
                TT(C, "dve", t1[:, half * 4:half * 4 + 4, :].rearrange("p (h j) t -> p h j t", j=2),
                   C.ps[bo[half]][:, :].rearrange("p (h j t) -> p h j t", h=2, j=2),
                   rs4[:, 2 * half:2 * half + 2, :].unsqueeze(2).to_broadcast([128, 2, 2, NT]), ALU.mult,
                   [C.PS[bo[half]], Brstd2], [Bt1])
            TT(C, "dve", t1[:], t1[:], C.gla_nw[:, 0:8].unsqueeze(2).to_broadcast([128, 8, NT]), ALU.mult, [Bt1, C.Bconst], [Bt1])
            TT(C, "pool", og[:], t1[:], silr[:], ALU.mult, [Bt1, Bsilr], [Bog])
            for half in range(2):
                by = C.bank()
                for mm_ in range(4):
                    m = half * 4 + mm_
                    for k in range(8):
                        MM(C, C.ps[by][:, mm_ * NT:(mm_ + 1) * NT], wo[:, k, m * 128:(m + 1) * 128], og[:, k, :], k == 0, k == 7,
                           [Bog] + Bwo, [C.PS[by]])
                TT(C, "dve", xt[:, half * 4:half * 4 + 4, :], xt[:, half * 4:half * 4 + 4, :],
                   C.ps[by][:, :].rearrange("p (m t) -> p m t", m=4), ALU.add, [C.PS[by], Bx], [Bx])
            C.release(bo[0])
            C.release(bo[1])
            DMA(C, dst[:, :, t0:t0 + NT], xt[:], [Bx], [])

        ntile = L // NT
        p1(0)
        for ti in range(ntile):
            lists = []
            if ti + 1 < ntile:
                C.bankset = [0, 1]
                P.capture()
                p1(ti + 1)
                lists.append(P.end_capture())
            C.bankset = [2, 3, 4, 5, 6, 7]
            P.capture()
            p2(ti)
            lists.append(P.end_capture())
            C.bankset = list(range(8))
            P.emit_interleaved(lists)
    P.barrier()


def sb_stage(C, src, dst):
    nc, P = C.nc, C.P
    osc = nc.dram_tensor("sb_osc", [8, 128, L], BF16).ap()
    with ExitStack() as st:
        sb = lambda name, shape, dt: st.enter_context(nc.sbuf_tensor(C.un(name), shape, dt))
        HT = sb("HT", [128, 8, L + 1], BF16)
        BHT = Buf("HT")
        with ExitStack() as st0:
            sb0 = lambda name, shape, dt: st0.enter_context(nc.sbuf_tensor(C.un(name), shape, dt))
            xts = [sb0("xt%d" % i, [128, 8, 512], F32) for i in range(2)]
            Bxt = [Buf("xt") for _ in range(2)]
            sqb = [sb0("sq%d" % i, [128, 512], BF16) for i in range(2)]
            Bsq = [Buf("sq") for _ in range(2)]
            rstd = sb0("rstd", [128, 512], F32)
            Brstd = Buf("rstd")
            MSET(C, "pool", HT[:, :, 0:1], 0.0, [BHT])
            for ti, (t0, T) in enumerate(tiles(512)):
                xt, Bx = xts[ti % 2], Bxt[ti % 2]
                DMA(C, xt[:, :, :T], src[:, :, t0:t0 + T], (), [Bx])
                norm_tile(C, xt, T, C.mix_nw[:, 3, :], HT[:, :, 1 + t0:1 + t0 + T], sqb, Bsq, rstd, Brstd, Bx, BHT)
        P.barrier()
        with ExitStack() as st1:
            sb1 = lambda name, shape, dt: st1.enter_context(nc.sbuf_tensor(C.un(name), shape, dt))
            wqk_t, Bwqk = load_w(C, st1, "wqk", C.din["sb_wqk"], 16 * 8 * 128)
            wv_t, Bwv = load_w(C, st1, "wv", C.din["sb_wv"], 8 * 8 * 128)
            wqk = wqk_t[:].rearrange("p (m k c) -> p m k c", m=16, k=8)
            wv = wv_t[:].rearrange("p (m k c) -> p m k c", m=8, k=8)
            wvn_t = sb1("wvn", [128, 8 * 8 * 128], BF16)
            wvn = wvn_t[:].rearrange("p (m k c) -> p m k c", m=8, k=8)
            Bwvn = Buf("wvn")
            C.P.op("pool", lambda e: e.tensor_scalar_mul(out=wvn_t[:], in0=wv_t[:], scalar1=-1.0), Bwv, [Bwvn])
            lti, Blti = load_rows(C, st1, "lti", C.din["sb_lti"], 128, 128, BF16)
            msk_t, Bmsk = load_rows(C, st1, "msk", C.din["sb_mask"], 128, 4 * 512, BF16)
            msk = msk_t[:].rearrange("p (d t) -> p d t", d=4)
            qT = sb1("qT", [128, L], BF16)
            kT = sb1("kT", [128, L], BF16)
            dv = sb1("dv", [128, 32, 128], BF16)
            BqT, BkT, Bdv = Buf("qT"), Buf("kT"), Buf("dv")
            Eb = [sb1("E%d" % i, [128, 2, 512], F32) for i in range(2)]
            BE = [Buf("E") for _ in range(2)]
            Lb = [sb1("L%d" % i, [128, 2, 512], BF16) for i in range(5)]
            BL = [Buf("L") for _ in range(5)]
            Pb = [sb1("P%d" % i, [128, 2, 512], BF16) for i in range(4)]
            BP = [Buf("P") for _ in range(4)]
            RC = [sb1("RC%d" % i, [128, 2, 512], F32) for i in range(2)]
            BRC = [Buf("RC") for _ in range(2)]
            Tt = sb1("Tt", [128, 512], F32)
            BT = Buf("T")
            oT = [sb1("oT%d" % i, [64, 512], BF16) for i in range(2)]
            BoT = [Buf("oT") for _ in range(2)]
            blk = 0
            oi = 0
            for pp in range(8):
                for nt in range(8):
                    for which in range(2):
                        b = C.bank()
                        for k in range(8):
                            MM(C, C.ps[b][:, :], wqk[:, which * 8 + pp, k, :], HT[:, k, 1 + 512 * nt:1 + 512 * nt + 512],
                               k == 0, k == 7, [BHT] + Bwqk, [C.PS[b]])
                        if which == 0:
                            ACT(C, qT[:, 512 * nt:512 * nt + 512], C.ps[b][:, :], AF.Identity, [C.PS[b]], [BqT], scale=0.125)
                        else:
                            CP(C, "dve", kT[:, 512 * nt:512 * nt + 512], C.ps[b][:, :], [C.PS[b]], [BkT])
                for q4 in range(8):
                    b = C.bank()
                    for i4 in range(4):
                        tt = 4 * q4 + i4
                        for k in range(8):
                            MM(C, C.ps[b][:, i4 * 128:(i4 + 1) * 128], HT[:, k, 128 * tt:128 * tt + 128], wv[:, pp, k, :],
                               k == 0, False, [BHT] + Bwv, [C.PS[b]])
                        for k in range(8):
                            MM(C, C.ps[b][:, i4 * 128:(i4 + 1) * 128], HT[:, k, 1 + 128 * tt:1 + 128 * tt + 128], wvn[:, pp, k, :],
                               False, k == 7, [BHT, Bwvn], [C.PS[b]])
                    CP(C, "dve", dv[:, 4 * q4:4 * q4 + 4, :].rearrange("p a c -> p (a c)"), C.ps[b][:, :], [C.PS[b]], [Bdv])
                plist = []
                for hh in range(2):
                    for qt in range(8):
                        nkb = 4 * qt + 4
                        for pi, kb in enumerate(range(nkb - 1, 0, -2)):
                            d = kb - 4 * qt
                            plist.append(dict(hh=hh, qt=qt, kb=kb, d=d, c0=(128 * (d - 1) if d >= 1 else 0),
                                              first=(pi == 0), last=(kb == 1)))
                pst = C.pst

                def stA(p, j):
                    hs = slice(64 * p["hh"], 64 * p["hh"] + 64)
                    c0, qt, kb = p["c0"], p["qt"], p["kb"]
                    zb = 2 * (j % 2)
                    for r in range(2):
                        MM(C, pst[:, zb + r, c0:512], kT[hs, 128 * (kb - r):128 * (kb - r) + 128], qT[hs, 512 * qt + c0:512 * qt + 512],
                           True, True, [BkT, BqT], [C.PS[zb + r]])
                    e2, l3 = j % 2, j % 5
                    ACT(C, Eb[e2][:, :, c0:512], pst[:, zb:zb + 2, c0:512], AF.Exp, [C.PS[zb], C.PS[zb + 1]], [BE[e2]])
                    ACT(C, Lb[l3][:, :, c0:512], Eb[e2][:, :, c0:512], AF.Ln, [BE[e2], C.Bconst], [BL[l3]], bias=C.onec[:, 0:1])
                    if p["d"] >= 1:
                        for r in range(2):
                            TT(C, "pool", Lb[l3][:, r, c0:512], Lb[l3][:, r, c0:512], msk[:, p["d"] - r, c0:512], ALU.mult,
                               [BL[l3], Bmsk], [BL[l3]])

                def stB(p, j):
                    c0 = p["c0"]
                    l3, p3, r2 = j % 5, j % 4, j % 2
                    L0, L1 = Lb[l3][:, 0, c0:512], Lb[l3][:, 1, c0:512]
                    MM(C, pst[:, 4, c0:512], lti[:], L0, True, True, [BL[l3], Blti], [C.PS[4]])
                    MM(C, pst[:, 5, c0:512], lti[:], L1, True, False, [BL[l3], Blti], [C.PS[5]])
                    MM(C, pst[:, 5, c0:512], C.ones[:], L0, False, True, [BL[l3], C.Bconst], [C.PS[5]])
                    if p["first"]:
                        MSET(C, "pool", Tt[:], 0.0, [BT])
                        ACT(C, Pb[p3][:, :, c0:512], pst[:, 4:6, c0:512], AF.Exp, [C.PS[4], C.PS[5]], [BP[p3]], scale=-1.0)
                    else:
                        TT(C, "dve", RC[r2][:, :, c0:512], pst[:, 4:6, c0:512],
                           Tt[:, c0:512].unsqueeze(1).to_broadcast([128, 2, 512 - c0]), ALU.add, [C.PS[4], C.PS[5], BT], [BRC[r2]])
                        ACT(C, Pb[p3][:, :, c0:512], RC[r2][:, :, c0:512], AF.Exp, [BRC[r2]], [BP[p3]], scale=-1.0)
                    if p["d"] >= 1:
                        for r in range(2):
                            TT(C, "pool", Pb[p3][:, r, c0:512], Pb[p3][:, r, c0:512], msk[:, p["d"] - r, c0:512], ALU.mult,
                               [BP[p3], Bmsk], [BP[p3]])
                    if not p["last"]:
                        MM(C, pst[:, 6, c0:512], C.ones[:], L0, True, False, [BL[l3], C.Bconst], [C.PS[6]])
                        MM(C, pst[:, 6, c0:512], C.ones[:], L1, False, True, [BL[l3], C.Bconst], [C.PS[6]])
                        TT(C, "dve", Tt[:, c0:512], Tt[:, c0:512], pst[:, 6, c0:512], ALU.add, [C.PS[6], BT], [BT])

                def stC(p, j):
                    hs = slice(64 * p["hh"], 64 * p["hh"] + 64)
                    c0, qt, kb, hh = p["c0"], p["qt"], p["kb"], p["hh"]
                    p3 = j % 4
                    if p["first"]:
                        for k in range(8):
                            MM(C, pst[0:64, 7, :], wv[:, pp, k, hs], HT[:, k, 512 * qt:512 * qt + 512], k == 0, False,
                               [BHT] + Bwv, [C.PS[7]])
                    MM(C, pst[0:64, 7, c0:512], dv[:, kb, hs], Pb[p3][:, 0, c0:512], False, False, [Bdv, BP[p3]], [C.PS[7]])
                    MM(C, pst[0:64, 7, c0:512], dv[:, kb - 1, hs], Pb[p3][:, 1, c0:512], False, p["last"], [Bdv, BP[p3]], [C.PS[7]])
                    if p["last"]:
                        o2 = (2 * pp + hh + qt) % 2
                        ACT(C, oT[o2][:], pst[0:64, 7, :], AF.Copy, [C.PS[7]], [BoT[o2]])
                        DMA(C, osc[pp, 64 * hh:64 * hh + 64, 512 * qt:512 * qt + 512], oT[o2][:], [BoT[o2]], [C.Bosc])

                npair = len(plist)
                SKA, SKB = SK_
                for i in range(-SKA, npair):
                    if 0 <= i + SKA < npair:
                        stA(plist[i + SKA], i + SKA)
                    if 0 <= i + SKB < npair:
                        stB(plist[i + SKB], i + SKB)
                    if 0 <= i < npair:
                        stC(plist[i], i)
        P.barrier()
    with ExitStack() as st:
        sb = lambda name, shape, dt: st.enter_context(nc.sbuf_tensor(C.un(name), shape, dt))
        wo_t, Bwo = load_w(C, st, "wo", C.din["sb_wo"], 8 * 1024)
        wo = wo_t[:].rearrange("p (k c) -> p k c", k=8)
        xts = [sb("xt%d" % i, [128, 8, 512], F32) for i in range(2)]
        Bxt = [Buf("xt") for _ in range(2)]
        ots = [sb("ot%d" % i, [128, 8, 512], BF16) for i in range(2)]
        Bot = [Buf("ot") for _ in range(2)]
        oscv = osc.rearrange("c p t -> p c t")
        tlc = tiles(512)

        def ldc(ti):
            t0, T = tlc[ti]
            DMA(C, xts[ti % 2][:], src[:, :, t0:t0 + T], (), [Bxt[ti % 2]])
            DMA(C, ots[ti % 2][:], oscv[:, :, t0:t0 + T], [C.Bosc], [Bot[ti % 2]])

        ldc(0)
        for ti, (t0, T) in enumerate(tlc):
            xt, Bx = xts[ti % 2], Bxt[ti % 2]
            ot, Bo = ots[ti % 2], Bot[ti % 2]
            if ti + 1 < len(tlc):
                ldc(ti + 1)
            for m in range(8):
                b = C.bank()
                for k in range(8):
                    MM(C, C.ps[b][:, :], wo[:, k, m * 128:(m + 1) * 128], ot[:, k, :], k == 0, k == 7, [Bo] + Bwo, [C.PS[b]])
                TT(C, "dve", xt[:, m, :], xt[:, m, :], C.ps[b][:, :], ALU.add, [C.PS[b], Bx], [Bx])
            DMA(C, dst[:, :, t0:t0 + T], xt[:], [Bx], [])
    P.barrier()


def ssd_stage(C, src, dst):
    nc, P = C.nc, C.P
    NT = 128
    with ExitStack() as st:
        sb = lambda name, shape, dt: st.enter_context(nc.sbuf_tensor(C.un(name), shape, dt))
        wz_t, Bwz = load_w(C, st, "wz", C.din["ssd_wz"], 4 * 8 * 512)
        wx_t, Bwx = load_w(C, st, "wx", C.din["ssd_wx"], 24 * 8 * 128)
        wdt_t, Bwdt = load_w(C, st, "wdt", C.din["ssd_wdt"], 8 * 32)
        wo_t, Bwo0 = load_w(C, st, "wo", C.din["ssd_wo"], 16 * 1024)
        wz = wz_t[:].rearrange("p (m k c) -> p m k c", m=4, k=8)
        wx = wx_t[:].rearrange("p (m k c) -> p m k c", m=24, k=8)
        wdt = wdt_t[:].rearrange("p (k c) -> p k c", k=8)
        wo = wo_t[:].rearrange("p (k c) -> p k c", k=16)
        tri, Btri = load_rows(C, st, "tri", C.din["ssd_tri"], 128, 384, F32)
        cwt, Bcw = load_rows(C, st, "cwt", C.din["ssd_cw"], 128, 96, F32)
        cw = cwt[:].rearrange("p (k m) -> p k m", k=4)
        cb, Bcb = load_rows(C, st, "cbt", C.din["ssd_cb"], 128, 24, F32)
        vec, Bvec = load_rows(C, st, "vec", C.din["ssd_vec"], 128, 96, F32)
        nw16, Bnw16 = load_rows(C, st, "nw16", C.din["ssd_nw"], 128, 16, F32)
        Bwo = [Buf("wo_f")]
        for k in range(16):
            C.P.op("pool", (lambda e, k=k: e.tensor_scalar_mul(out=wo[:, k, :], in0=wo[:, k, :], scalar1=nw16[:, k:k + 1])),
                   Bwo0 + [Bnw16], Bwo)
        ab = sb("ab", [128, 32], F32)
        Bab = Buf("ab")
        ACT(C, ab[:], vec[:, 32:64], AF.Exp, [Bvec], [Bab])
        C.P.op("pool", lambda e: e.tensor_scalar_mul(out=ab[:], in0=ab[:], scalar1=-1.0), [Bab], [Bab])
        TIN = tri[:, 0:128]
        TSR = tri[:, 128:256]
        ONE = tri[:, 256:384]
        xts = [sb("xt%d" % i, [128, 8, NT], F32) for i in range(2)]
        Bxs = [Buf("xt") for _ in range(2)]
        hTs = [sb("hT%d" % i, [128, 8, NT], BF16) for i in range(2)]
        Bhs = [Buf("hT") for _ in range(2)]
        sqb = [sb("sq%d" % i, [128, NT], BF16) for i in range(2)]
        Bsq = [Buf("sq") for _ in range(2)]
        rstd = sb("rstd", [128, NT], F32)
        Brstd = Buf("rstd")
        dtp = sb("dtp", [128, 32], F32)
        Bdtp = Buf("dtp")
        dtts = [sb("dtt%d" % i, [128, 32], F32) for i in range(2)]
        dtAs = [sb("dtA%d" % i, [128, 32], F32) for i in range(2)]
        Bdts = [Buf("dt") for _ in range(2)]
        BdtAs = [Buf("dtA") for _ in range(2)]
        exs = [sb("ex%d" % i, [128, 96], F32) for i in range(2)]
        Bexs = [Buf("ex") for _ in range(2)]
        Pb = sb("Pb", [128, 4, NT + 3], F32)
        BPb = Buf("Pb")
        halo = sb("halo", [128, 24, 3], F32)
        Bhalo = Buf("halo")
        a0 = sb("a0", [128, 4, NT], F32)
        a1 = sb("a1", [128, 4, NT], F32)
        Ba0, Ba1 = Buf("a0"), Buf("a1")
        XSb = sb("XSb", [128, 16, NT], BF16)
        BXS = Buf("XSb")
        BCTs = [sb("BCT%d" % i, [128, 8, NT], BF16) for i in range(2)]
        BBCTs = [Buf("BCT") for _ in range(2)]
        xs_toks = [sb("xs_tok%d" % i, [128, 2048], BF16) for i in range(2)]
        Bxsts = [Buf("xs_tok") for _ in range(2)]
        xdts = [sb("xdt%d" % i, [128, 32, 64], BF16) for i in range(2)]
        Bxdts = [Buf("xdt") for _ in range(2)]
        Btoks = [sb("Btok%d" % i, [128, 512], BF16) for i in range(2)]
        BBtoks = [Buf("Btok") for _ in range(2)]
        Rbs = [sb("Rb%d" % i, [128, 8, NT], F32) for i in range(2)]
        BRbs = [Buf("Rb") for _ in range(2)]
        LM = sb("LM", [128, 8, NT], BF16)
        BLM = Buf("LM")
        cbms = [sb("cbm%d" % i, [128, NT], F32) for i in range(2)]
        Bcbms = [Buf("cbm") for _ in range(2)]
        Mg = [sb("Mg%d" % i, [128, 8, NT], BF16) for i in range(2)]
        BMg = [Buf("Mg") for _ in range(2)]
        ybs = [sb("yb%d" % i, [128, 512], F32) for i in range(2)]
        Bybs = [Buf("yb") for _ in range(2)]
        Sst = sb("Sst", [128, 2048], F32)
        BSst = [Buf("Sst") for _ in range(4)]
        Sbf = sb("Sbf", [128, 2048], BF16)
        BSbf = [Buf("Sbf") for _ in range(4)]
        tmpy = sb("tmpy", [128, 512], F32)
        Btmpy = Buf("tmpy")
        zs = sb("zs", [128, 512], F32)
        Bzs = Buf("zs")
        ssq = sb("ssq", [128, 4], F32)
        Bssq = Buf("ssq")
        yns = [sb("yn%d" % i, [128, 512], BF16) for i in range(2)]
        Byns = [Buf("yn") for _ in range(2)]
        ynT = sb("ynT", [128, 16, NT], BF16)
        BynT = Buf("ynT")
        MSET(C, "pool", halo[:], 0.0, [Bhalo])
        MSET(C, "pool", Sst[:], 0.0, BSst)
        MSET(C, "pool", Sbf[:], 0.0, BSbf)
        nw = C.mix_nw[:, 2, :]
        v3 = lambda ap, a: ap.rearrange("p (a b) -> p a b", a=a)
        def p1(ti):
            t0 = ti * NT
            xt, Bx = xts[ti % 2], Bxs[ti % 2]
            hT, Bh = hTs[ti % 2], Bhs[ti % 2]
            dtt, Bdt = dtts[ti % 2], Bdts[ti % 2]
            dtA, BdtA = dtAs[ti % 2], BdtAs[ti % 2]
            ex, Bex = exs[ti % 2], Bexs[ti % 2]
            BCT, BBCT = BCTs[ti % 2], BBCTs[ti % 2]
            xs_tok, Bxst = xs_toks[ti % 2], Bxsts[ti % 2]
            xdt, Bxdt = xdts[ti % 2], Bxdts[ti % 2]
            Btok, BBtok = Btoks[ti % 2], BBtoks[ti % 2]
            DMA(C, xt[:], src[:, :, t0:t0 + NT], (), [Bx])
            norm_tile(C, xt, NT, nw, hT, sqb, Bsq, rstd, Brstd, Bx, Bh)
            b = C.bank()
            for k in range(8):
                MM(C, C.ps[b][:, 0:32], hT[:, k, :], wdt[:, k, :], k == 0, k == 7, [Bh] + Bwdt, [C.PS[b]])
            TT(C, "dve", dtp[:], C.ps[b][:, 0:32], vec[:, 0:32], ALU.add, [C.PS[b], Bvec], [Bdtp])
            ACT(C, dtp[:], dtp[:], AF.Exp, [Bdtp], [Bdtp])
            ACT(C, dtt[:], dtp[:], AF.Ln, [Bdtp, C.Bconst], [Bdt], bias=C.onec[:, 0:1])
            TT(C, "dve", dtA[:], dtt[:], ab[:], ALU.mult, [Bdt, Bab], [BdtA])
            bc = C.bank()
            MM(C, C.ps[bc][:, 0:32], TIN, dtA[:], True, True, [BdtA, Btri], [C.PS[bc]])
            MM(C, C.ps[bc][:, 32:64], TSR, dtA[:], True, True, [BdtA, Btri], [C.PS[bc]])
            MM(C, C.ps[bc][:, 64:96], ONE, dtA[:], True, True, [BdtA, Btri], [C.PS[bc]])
            ACT(C, ex[:], C.ps[bc][:, 0:96], AF.Exp, [C.PS[bc]], [Bex])
            for q in range(6):
                CP(C, "pool", Pb[:, :, 0:3], halo[:, 4 * q:4 * q + 4, :], [Bhalo], [BPb])
                b = C.bank()
                for mm_ in range(4):
                    m = 4 * q + mm_
                    for k in range(8):
                        MM(C, C.ps[b][:, mm_ * NT:(mm_ + 1) * NT], wx[:, m, k, :], hT[:, k, :], k == 0, k == 7, [Bh] + Bwx, [C.PS[b]])
                ACT(C, Pb[:, :, 3:NT + 3], v3(C.ps[b][:, :], 4), AF.Copy, [C.PS[b]], [BPb])
                CP(C, "pool", halo[:, 4 * q:4 * q + 4, :], Pb[:, :, NT:NT + 3], [BPb], [Bhalo])
                wb = lambda kk: cw[:, kk, 4 * q:4 * q + 4].unsqueeze(2).to_broadcast([128, 4, NT])
                TT(C, "dve", a0[:], Pb[:, :, 3:NT + 3], wb(3), ALU.mult, [BPb, Bcw], [Ba0])
                TT(C, "pool", a1[:], Pb[:, :, 2:NT + 2], wb(2), ALU.mult, [BPb, Bcw], [Ba1])
                TT(C, "dve", a0[:], a0[:], a1[:], ALU.add, [Ba0, Ba1], [Ba0])
                TT(C, "pool", a1[:], Pb[:, :, 1:NT + 1], wb(1), ALU.mult, [BPb, Bcw], [Ba1])
                TT(C, "dve", a0[:], a0[:], a1[:], ALU.add, [Ba0, Ba1], [Ba0])
                TT(C, "pool", a1[:], Pb[:, :, 0:NT], wb(0), ALU.mult, [BPb, Bcw], [Ba1])
                TT(C, "dve", a0[:], a0[:], a1[:], ALU.add, [Ba0, Ba1], [Ba0])
                TT(C, "dve", a0[:], a0[:], cb[:, 4 * q:4 * q + 4].unsqueeze(2).to_broadcast([128, 4, NT]), ALU.add, [Ba0, Bcb], [Ba0])
                if q < 4:
                    ACT(C, XSb[:, 4 * q:4 * q + 4, :], a0[:], AF.Silu, [Ba0], [BXS])
                else:
                    ACT(C, BCT[:, 4 * (q - 4):4 * (q - 4) + 4, :], a0[:], AF.Silu, [Ba0], [BBCT])
            for half in range(2):
                b = C.bank()
                psb = C.ps[b][:].bitcast(BF16)
                for i in range(8):
                    TR(C, psb[:, i * 128:(i + 1) * 128], XSb[:, 8 * half + i, :], [BXS], [C.PS[b]])
                CP(C, "dve", xs_tok[:, half * 1024:(half + 1) * 1024], psb[:, :], [C.PS[b]], [Bxst])
                TT(C, "dve", xdt[:, 16 * half:16 * half + 16, :], v3(psb[:, :], 16),
                   dtt[:, 16 * half:16 * half + 16].unsqueeze(2).to_broadcast([128, 16, 64]), ALU.mult, [C.PS[b], Bdt], [Bxdt])
            b = C.bank()
            psb = C.ps[b][:].bitcast(BF16)
            for g_ in range(4):
                TR(C, psb[:, g_ * 128:(g_ + 1) * 128], BCT[:, g_, :], [BBCT], [C.PS[b]])
            ACT(C, Btok[:], psb[:, 0:512], AF.Copy, [C.PS[b]], [BBtok])

        def p2(ti):
            t0 = ti * NT
            xt, Bx = xts[ti % 2], Bxs[ti % 2]
            hT, Bh = hTs[ti % 2], Bhs[ti % 2]
            dtA, BdtA = dtAs[ti % 2], BdtAs[ti % 2]
            ex, Bex = exs[ti % 2], Bexs[ti % 2]
            BCT, BBCT = BCTs[ti % 2], BBCTs[ti % 2]
            xs_tok, Bxst = xs_toks[ti % 2], Bxsts[ti % 2]
            xdt, Bxdt = xdts[ti % 2], Bxdts[ti % 2]
            Btok, BBtok = Btoks[ti % 2], BBtoks[ti % 2]
            by = [C.hold() for _ in range(4)]
            for g_ in range(4):
                Rb, BRb = Rbs[g_ % 2], BRbs[g_ % 2]
                cbm, Bcbm = cbms[g_ % 2], Bcbms[g_ % 2]
                TT(C, "dve", Rb[:], dtA[:, 8 * g_:8 * g_ + 8].unsqueeze(2).to_broadcast([128, 8, NT]),
                   TIN.unsqueeze(1).to_broadcast([128, 8, NT]), ALU.mult, [BdtA, Btri], [BRb])
                Rf = Rb[:].rearrange("p a b -> p (a b)")
                for hf in range(2):
                    bs_ = C.bank()
                    MM(C, C.ps[bs_][:, :], TSR, Rf[:, hf * 512:(hf + 1) * 512], True, True, [BRb, Btri], [C.PS[bs_]])
                    ACT(C, LM[:, 4 * hf:4 * hf + 4, :], v3(C.ps[bs_][:, :], 4), AF.Exp, [C.PS[bs_]], [BLM])
                bcb = C.bank()
                MM(C, C.ps[bcb][:, 0:NT], BCT[:, g_, :], BCT[:, 4 + g_, :], True, True, [BBCT], [C.PS[bcb]])
                TT(C, "dve", cbm[:], C.ps[bcb][:, 0:NT], TIN, ALU.mult, [C.PS[bcb], Btri], [Bcbm])
                mg, Bm = Mg[g_ % 2], BMg[g_ % 2]
                TT(C, "dve", mg[:], LM[:], cbm[:].unsqueeze(1).to_broadcast([128, 8, NT]), ALU.mult, [BLM, Bcbm], [Bm])
                for hh in range(8):
                    MM(C, C.ps[by[g_]][:, hh * 64:(hh + 1) * 64], mg[:, hh, :], xdt[:, 8 * g_ + hh, :], True, True, [Bm, Bxdt], [C.PS[by[g_]]])
            TT(C, "pool", xdt[:], xdt[:], ex[:, 32:64].unsqueeze(2).to_broadcast([128, 32, 64]), ALU.mult, [Bxdt, Bex], [Bxdt])
            MSET(C, "pool", ssq[:], 0.0, [Bssq])
            for g_ in range(4):
                gs = slice(512 * g_, 512 * g_ + 512)
                yb, Byb = ybs[g_ % 2], Bybs[g_ % 2]
                yn, Byn = yns[g_ % 2], Byns[g_ % 2]
                b = C.bank()
                MM(C, C.ps[b][:, :], BCT[:, 4 + g_, :], Sbf[:, gs], True, True, [BBCT, BSbf[g_]], [C.PS[b]])
                TT(C, "dve", v3(yb[:], 8), v3(C.ps[b][:, :], 8), ex[:, 8 * g_:8 * g_ + 8].unsqueeze(2).to_broadcast([128, 8, 64]),
                   ALU.mult, [C.PS[b], Bex], [Byb])
                TT(C, "dve", yb[:], yb[:], C.ps[by[g_]][:, :], ALU.add, [C.PS[by[g_]], Byb], [Byb])
                C.release(by[g_])
                b = C.bank()
                MM(C, C.ps[b][:, :], Btok[:, g_ * 128:(g_ + 1) * 128], xdt[:, 8 * g_:8 * g_ + 8, :].rearrange("p a b -> p (a b)"), True, True,
                   [BBtok, Bxdt], [C.PS[b]])
                TT(C, "pool", v3(Sst[:, gs], 8), v3(Sst[:, gs], 8), ex[:, 64 + 8 * g_:64 + 8 * g_ + 8].unsqueeze(2).to_broadcast([128, 8, 64]),
                   ALU.mult, [BSst[g_], Bex], [BSst[g_]])
                TT(C, "dve", Sst[:, gs], Sst[:, gs], C.ps[b][:, :], ALU.add, [C.PS[b], BSst[g_]], [BSst[g_]])
                CP(C, "pool", Sbf[:, gs], Sst[:, gs], [BSst[g_]], [BSbf[g_]])
                TT(C, "pool", v3(tmpy[:], 8), v3(xs_tok[:, gs], 8), vec[:, 64 + 8 * g_:64 + 8 * g_ + 8].unsqueeze(2).to_broadcast([128, 8, 64]),
                   ALU.mult, [Bxst, Bvec], [Btmpy])
                TT(C, "dve", yb[:], yb[:], tmpy[:], ALU.add, [Btmpy, Byb], [Byb])
                b = C.bank()
                for k in range(8):
                    MM(C, C.ps[b][:, :], hT[:, k, :], wz[:, g_, k, :], k == 0, k == 7, [Bh] + Bwz, [C.PS[b]])
                ACT(C, zs[:], C.ps[b][:, :], AF.Silu, [C.PS[b]], [Bzs])
                TT(C, "pool", yb[:], yb[:], zs[:], ALU.mult, [Bzs, Byb], [Byb])
                ACT(C, zs[:], yb[:], AF.Square, [Byb, Bzs, Bssq], [Bzs, Bssq], accum_out=ssq[:, g_:g_ + 1])
                ACT(C, ssq[:, g_:g_ + 1], ssq[:, g_:g_ + 1], AF.Ln, [Bssq, C.Bconst], [Bssq], scale=1.0 / 512, bias=C.epsc[:, 0:1])
                ACT(C, ssq[:, g_:g_ + 1], ssq[:, g_:g_ + 1], AF.Exp, [Bssq], [Bssq], scale=-0.5)
                C.P.op("dve", (lambda e, yn=yn, yb=yb, g_=g_: e.tensor_scalar_mul(out=yn[:], in0=yb[:], scalar1=ssq[:, g_:g_ + 1])),
                       [Byb, Bssq], [Byn])
                b = C.bank()
                psb = C.ps[b][:].bitcast(BF16)
                for i in range(4):
                    TR(C, psb[:, i * 128:(i + 1) * 128], yn[:, i * 128:(i + 1) * 128], [Byn], [C.PS[b]])
                ACT(C, ynT[:, 4 * g_:4 * g_ + 4, :].rearrange("p a b -> p (a b)"), psb[:, 0:512], AF.Copy, [C.PS[b]], [BynT])
            for half in range(2):
                b = C.bank()
                for mm_ in range(4):
                    m = 4 * half + mm_
                    for k in range(16):
                        MM(C, C.ps[b][:, mm_ * NT:(mm_ + 1) * NT], wo[:, k, m * 128:(m + 1) * 128], ynT[:, k, :], k == 0, k == 15,
                           [BynT] + Bwo, [C.PS[b]])
                TT(C, "dve", xt[:, 4 * half:4 * half + 4, :], xt[:, 4 * half:4 * half + 4, :], v3(C.ps[b][:, :], 4), ALU.add, [C.PS[b], Bx], [Bx])
            DMA(C, dst[:, :, t0:t0 + NT], xt[:], [Bx], [])

        ntile = L // NT
        p1(0)
        for ti in range(ntile):
            lists = []
            if ti + 1 < ntile:
                C.bankset = [0, 1]
                P.capture()
                p1(ti + 1)
                lists.append(P.end_capture())
            C.bankset = [2, 3, 4, 5, 6, 7]
            P.capture()
            p2(ti)
            lists.append(P.end_capture())
            C.bankset = list(range(8))
            P.emit_interleaved(lists)
    P.barrier()

SHARED_SPECS = {}


def build_program(stages, shared_shapes):
    nc = bass.Bass("TRN2", target_bir_lowering=False)
    C = Ctx()
    C.nc = nc
    C.P = Prog(nc)
    din = {}
    for name, (shape, dt) in shared_shapes.items():
        din[name] = nc.dram_tensor(name, list(shape), dt, kind="ExternalInput").ap()
    C.din = din
    C.uid = [0]

    def un(name):
        C.uid[0] += 1
        return "%s_%d" % (name, C.uid[0])
    C.un = un
    xT = nc.dram_tensor("xT", [D, L], F32, kind="ExternalInput").ap()
    out = nc.dram_tensor("out", [D, L], F32, kind="ExternalOutput").ap()
    r = [nc.dram_tensor("r%d" % i, [D, L], F32).ap() for i in range(2)]
    rv = lambda ap: ap.rearrange("(c p) t -> p c t", p=128)
    with ExitStack() as st:
        sb = lambda name, shape, dt: st.enter_context(nc.sbuf_tensor(C.un(name), shape, dt))
        C.pst = st.enter_context(nc.psum_tensor("pst", [128, 8, 512], F32))
        C.ps = [C.pst[:, i, :] for i in range(8)]
        C.PS = [Buf("ps%d" % i) for i in range(8)]
        C.bk = 0

        C.held = set()
        C.bankset = list(range(8))
        C.bkc = {}

        def bank():
            key = tuple(C.bankset)
            while True:
                c = C.bkc.get(key, 0)
                C.bkc[key] = c + 1
                i = C.bankset[c % len(C.bankset)]
                if i not in C.held:
                    return i

        def hold():
            i = bank()
            C.held.add(i)
            return i
        C.bank = bank
        C.hold = hold
        C.release = lambda i: C.held.discard(i)
        C.Bconst = Buf("const")
        C.ones = sb("ones", [128, 128], BF16)
        C.ident = sb("ident", [128, 128], BF16)
        DMA(C, C.ones[:], din["c_ones"], (), [C.Bconst], eng="pool")
        DMA(C, C.ident[:], din["c_ident"], (), [C.Bconst], eng="pool")
        C.epsc = sb("epsc", [128, 1], F32)
        C.onec = sb("onec", [128, 1], F32)
        MSET(C, "pool", C.epsc[:], EPS, [C.Bconst])
        MSET(C, "pool", C.onec[:], 1.0, [C.Bconst])

        def small(name, n):
            t = sb("s_" + name, [128, n], F32)
            DMA(C, t[:], din[name], (), [C.Bconst])
            return t
        C.mix_nw = small("mix_nw", 32)[:].rearrange("p (l k) -> p l k", l=4)
        C.ffn_nw = small("ffn_nw", 32)[:].rearrange("p (l k) -> p l k", l=4)
        C.fin_nw = small("fin_nw", 8)
        C.ffn_cw = small("ffn_cw", 4 * 3 * 44)[:].rearrange("p (l k m) -> p l k m", l=4, k=3)
        C.ffn_cb = small("ffn_cb", 4 * 44)[:].rearrange("p (l m) -> p l m", l=4)
        C.pool_b = small("pool_b", 8)
        C.pool_s = small("pool_s", 8)
        C.gla_nw = small("gla_nw", 8)
        C.Bosc = Buf('osc')
        C.P.barrier()
        cur = rv(xT)
        nxt_i = 0
        for si, sname in enumerate(stages):
            last = si == len(stages) - 1
            dst = rv(out) if last else rv(r[nxt_i])
            kind = sname[0]
            if kind == "ffn":
                ffn_stage(C, sname[1], cur, dst, final=(len(sname) > 2 and sname[2] == "final"))
            elif kind == "pool":
                pool_stage(C, cur, dst)
            elif kind == "gla":
                gla_stage(C, cur, dst)
            elif kind == "sb":
                sb_stage(C, cur, dst)
            elif kind == "ssd":
                ssd_stage(C, cur, dst)
            elif kind == "final":
                final_norm_stage(C, cur, dst)
            else:
                raise ValueError(kind)
            cur = dst
            nxt_i ^= 1
        C.P.emit()
    return nc


def wl(Wm, chunk):
    K, Fd = Wm.shape
    kc, m = K // 128, Fd // chunk
    return np.ascontiguousarray(Wm.reshape(kc, 128, m, chunk).transpose(1, 2, 0, 3).reshape(128, -1))


def pt(v):
    n = v.shape[-1] // 128
    return np.ascontiguousarray(np.moveaxis(v.reshape(v.shape[:-1] + (n, 128)), -1, 0).reshape(128, -1))


def prep_shared(inp):
    f = lambda a: np.asarray(a, dtype=np.float32)
    sh = {}
    sh["c_ones"] = np.ones((128, 128), np.float32)
    sh["c_ident"] = np.eye(128, dtype=np.float32)
    sh["mix_nw"] = pt(f(inp["mix_norm_w"]))
    sh["ffn_nw"] = pt(f(inp["ffn_norm_w"]))
    sh["fin_nw"] = pt(f(inp["final_norm_w"]))
    sh["ffn_cw"] = pt(f(inp["ffn_conv_w"]))
    sh["ffn_cb"] = pt(f(inp["ffn_conv_b"]))
    sh["ffn_wup"] = np.stack([wl(f(inp["ffn_w_up"][i]), 128) for i in range(4)])
    sh["ffn_wdn"] = np.stack([wl(f(inp["ffn_w_down"][i]), 1024) for i in range(4)])
    pw = f(inp["pool_w"][0])
    sh["pool_w"] = np.ascontiguousarray(pw.reshape(4, 2, 128, 256).transpose(2, 0, 1, 3).reshape(128, -1))
    sh["pool_b"] = pt(f(inp["pool_b"][0]).reshape(-1))
    sh["pool_s"] = pt(f(inp["pool_scale"][0]))
    W = TS + 15
    inv = np.zeros((4, W), np.float32)
    for g in range(4):
        win = 2 << g
        t = np.arange(W) - 15
        inv[g] = np.where(t >= 0, 1.0 / np.minimum(t + 1, win).clip(1), 0.0)
    sh["pool_inv"] = np.ascontiguousarray(np.broadcast_to(inv.reshape(1, -1), (128, 4 * W)))
    gw = f(inp["gla_w_in"][0])
    sh["gla_wq"] = wl(gw[:, 0:512], 128)
    sh["gla_wk"] = wl(gw[:, 512:1024], 512)
    sh["gla_wv"] = wl(gw[:, 1024:2048], 512)
    sh["gla_wr"] = wl(gw[:, 2048:3072], 128)
    sh["gla_wg"] = wl(gw[:, 3072:3088], 16)
    sh["gla_wo"] = wl(f(inp["gla_w_out"][0]), 1024)
    sh["gla_g2"] = np.concatenate([f(inp["gla_w_gate2"][0]), f(inp["gla_b_gate"][0])[None, :]], 0)
    sh["gla_nw"] = pt(f(inp["gla_norm_w"][0]))
    s_ = np.arange(128)
    same = (s_[:, None] // 64) == (s_[None, :] // 64)
    sh["gla_m1"] = np.where(same & (s_[:, None] > s_[None, :]), -1.0 / 16, 0.0).astype(np.float32)
    sh["gla_ncol"] = np.full((128, 1), -1.0 / 16, np.float32)
    sw = f(inp["sb_w_qkv"][0])
    sh["sb_wqk"] = wl(sw[:, 0:2048], 128)
    sh["sb_wv"] = wl(sw[:, 2048:3072], 128)
    sh["sb_wo"] = wl(f(inp["sb_w_out"][0]), 1024)
    sh["sb_lti"] = (s_[:, None] >= s_[None, :]).astype(np.float32)
    tq = np.arange(512)
    sh["sb_mask"] = np.concatenate([((128 * dd + s_[:, None]) < tq[None, :]).astype(np.float32) for dd in range(4)], 1)
    zw = f(inp["ssd_w_in"][0])
    sh["ssd_wz"] = wl(zw[:, 0:2048], 512)
    sh["ssd_wx"] = wl(zw[:, 2048:5120], 128)
    sh["ssd_wdt"] = wl(zw[:, 5120:5152], 32)
    sh["ssd_wo"] = wl(f(inp["ssd_w_out"][0]), 1024)
    sh["ssd_cw"] = pt(f(inp["ssd_conv_w"][0]))
    sh["ssd_cb"] = pt(f(inp["ssd_conv_b"][0]))
    sh["ssd_vec"] = np.ascontiguousarray(np.broadcast_to(np.concatenate(
        [f(inp["ssd_dt_bias"][0]), f(inp["ssd_a_log"][0]), f(inp["ssd_d"][0])])[None, :], (128, 96)))
    sh["ssd_nw"] = pt(f(inp["ssd_norm_w"][0]))
    sh["ssd_tri"] = np.concatenate([(s_[:, None] <= s_[None, :]).astype(np.float32),
                                    (s_[:, None] > s_[None, :]).astype(np.float32),
                                    np.ones((128, 128), np.float32)], 1)
    return sh


ALL_STAGES = [("gla",), ("ffn", 0), ("pool",), ("ffn", 1), ("ssd",), ("ffn", 2), ("sb",), ("ffn", 3, "final")]


def run_stages(inputs, stages, core_ids, xs):
    sh = prep_shared(inputs)
    shapes = {k: (v.shape, F32) for k, v in sh.items()}
    nc = build_program(stages, shapes)
    in_maps = []
    for x1 in xs:
        m = dict(sh)
        m["xT"] = np.ascontiguousarray(np.asarray(x1, np.float32).T)
        in_maps.append(m)
    res = run_bass_kernel_spmd(nc, in_maps, core_ids=core_ids)
    return [np.ascontiguousarray(r["out"].T) for r in res.results]


def kernel(**inputs):
    x = np.asarray(inputs["x"], np.float32)
    outs = run_stages(inputs, ALL_STAGES, list(range(8)), [x[b] for b in range(8)])
    return np.stack(outs, 0).astype(np.float32)
```
